# Optimizing a Trainium2 kernel written in Bass

```python
import jax, jax.numpy as jnp
from jax import lax
import numpy as np

D_MODEL = 1024
BATCH = 2
SEQ = 8192
DEPTH = 1

N_Q_HEADS = 8
N_KV_HEADS = 2
HEAD_DIM = 64
ATTN_WIDTH = N_Q_HEADS * HEAD_DIM
KV_WIDTH = N_KV_HEADS * HEAD_DIM
WINDOW = 128
BLOCK = 128
GMLP_GROUPS = 8
GMLP_GROUP_DIM = 64
GMLP_WIDTH = GMLP_GROUPS * GMLP_GROUP_DIM
CHUNK = 128
MIX_WIDTH = ATTN_WIDTH + GMLP_WIDTH
IN_WIDTH = ATTN_WIDTH + 2 * KV_WIDTH + 2 * GMLP_WIDTH
D_FF = 2816
FFN_RES = 0.5
EPS = 1e-6

kernel_name = "hybrid_swa_sink_gmlp_macaron"


def rmsnorm(x, g):
    xf = x.astype(jnp.float32)
    y = xf * lax.rsqrt(jnp.mean(xf * xf, axis=-1, keepdims=True) + EPS) * g.astype(jnp.float32)
    return y.astype(x.dtype)


def layernorm(x, g, b):
    xf = x.astype(jnp.float32)
    mu = jnp.mean(xf, axis=-1, keepdims=True)
    var = jnp.mean(jnp.square(xf - mu), axis=-1, keepdims=True)
    y = (xf - mu) * lax.rsqrt(var + EPS) * g.astype(jnp.float32) + b.astype(jnp.float32)
    return y.astype(x.dtype)


def swiglu(h, w_gate, w_up, w_down):
    return (jax.nn.silu(h @ w_gate) * (h @ w_up)) @ w_down


def sliding_window_sink_attention(q, k, v, sinks):
    b, s = q.shape[0], q.shape[1]
    nb = s // BLOCK
    rep = N_Q_HEADS // N_KV_HEADS
    qb = q.reshape(b, nb, BLOCK, N_KV_HEADS, rep, HEAD_DIM)

    def band(t):
        tb = t.reshape(b, nb, BLOCK, N_KV_HEADS, HEAD_DIM)
        prev = jnp.pad(tb, ((0, 0), (1, 0), (0, 0), (0, 0), (0, 0)))[:, :-1]
        return jnp.concatenate([prev, tb], axis=2)

    kw, vw = band(k), band(v)
    scores = jnp.einsum('bnqgrd,bnkgd->bngrqk', qb, kw,
                        preferred_element_type=jnp.float32) * (HEAD_DIM ** -0.5)
    qpos = jnp.arange(BLOCK)[:, None]
    kpos = jnp.arange(2 * BLOCK)[None, :] - BLOCK
    rel = qpos - kpos
    in_window = (rel >= 0) & (rel < WINDOW)
    real_key = (kpos >= 0)[None] | (jnp.arange(nb) > 0)[:, None, None]
    mask = (in_window[None] & real_key)[None, :, None, None]
    scores = jnp.where(mask, scores, -jnp.inf)
    sink = sinks.astype(jnp.float32).reshape(N_KV_HEADS, rep)[None, None, :, :, None, None]
    m = jnp.maximum(jnp.max(scores, axis=-1, keepdims=True), sink)
    p = jnp.exp(scores - m)
    p = p / (jnp.sum(p, axis=-1, keepdims=True) + jnp.exp(sink - m))
    out = jnp.einsum('bngrqk,bnkgd->bnqgrd', p.astype(v.dtype), vw)
    return out.reshape(b, s, ATTN_WIDTH)


def chunked_spatial_gating(z, ln_g, ln_b, w_s, b_s):
    u, vv = jnp.split(z, 2, axis=-1)
    vv = layernorm(vv, ln_g, ln_b)
    b, s = vv.shape[0], vv.shape[1]
    nc = s // CHUNK
    vc = vv.reshape(b, nc, CHUNK, GMLP_GROUPS, GMLP_GROUP_DIM)
    causal = jnp.tril(jnp.ones((CHUNK, CHUNK), dtype=bool))
    w = jnp.where(causal[None], w_s, 0).astype(vc.dtype)
    mixed = jnp.einsum('gts,bnsge->bntge', w, vc) + b_s.T.astype(vc.dtype)[None, None, :, :, None]
    return u * mixed.reshape(b, s, GMLP_WIDTH)


def setup_inputs(seed: int = 0) -> dict:
    key = jax.random.key(seed)
    ks = jax.random.split(key, 24)
    f32 = jnp.float32
    L = DEPTH

    def nrm(k, shape, scale):
        return jax.random.normal(k, shape, f32) * scale

    def gain(k, n):
        return 1.0 + 0.02 * jax.random.normal(k, (L, n), f32)

    return {
        "x": jax.random.normal(ks[0], (BATCH, SEQ, D_MODEL), f32),
        "ffn1_norm_g": gain(ks[1], D_MODEL),
        "ffn1_w_gate": nrm(ks[2], (L, D_MODEL, D_FF), D_MODEL ** -0.5),
        "ffn1_w_up": nrm(ks[3], (L, D_MODEL, D_FF), D_MODEL ** -0.5),
        "ffn1_w_down": nrm(ks[4], (L, D_FF, D_MODEL), D_FF ** -0.5),
        "mix_norm_g": gain(ks[5], D_MODEL),
        "w_in": nrm(ks[6], (L, D_MODEL, IN_WIDTH), D_MODEL ** -0.5),
        "b_in": nrm(ks[7], (L, IN_WIDTH), 0.02),
        "attn_sinks": nrm(ks[8], (L, N_Q_HEADS), 1.0),
        "gmlp_ln_g": gain(ks[9], GMLP_WIDTH),
        "gmlp_ln_b": nrm(ks[10], (L, GMLP_WIDTH), 0.02),
        "gmlp_w_s": nrm(ks[11], (L, GMLP_GROUPS, CHUNK, CHUNK), CHUNK ** -0.5),
        "gmlp_b_s": 1.0 + nrm(ks[12], (L, GMLP_GROUPS, CHUNK), 0.02),
        "attn_out_norm_g": gain(ks[13], ATTN_WIDTH),
        "gmlp_out_norm_g": gain(ks[14], GMLP_WIDTH),
        "w_out": nrm(ks[15], (L, MIX_WIDTH, D_MODEL), MIX_WIDTH ** -0.5),
        "b_out": nrm(ks[16], (L, D_MODEL), 0.02),
        "ffn2_norm_g": gain(ks[17], D_MODEL),
        "ffn2_w_gate": nrm(ks[18], (L, D_MODEL, D_FF), D_MODEL ** -0.5),
        "ffn2_w_up": nrm(ks[19], (L, D_MODEL, D_FF), D_MODEL ** -0.5),
        "ffn2_w_down": nrm(ks[20], (L, D_FF, D_MODEL), D_FF ** -0.5),
        "final_norm_g": 1.0 + 0.02 * jax.random.normal(ks[21], (D_MODEL,), f32),
    }


def reference(x, ffn1_norm_g, ffn1_w_gate, ffn1_w_up, ffn1_w_down, mix_norm_g, w_in, b_in,
              attn_sinks, gmlp_ln_g, gmlp_ln_b, gmlp_w_s, gmlp_b_s, attn_out_norm_g,
              gmlp_out_norm_g, w_out, b_out, ffn2_norm_g, ffn2_w_gate, ffn2_w_up, ffn2_w_down,
              final_norm_g):
    b, s, _ = x.shape
    for l in range(DEPTH):
        x = x + FFN_RES * swiglu(rmsnorm(x, ffn1_norm_g[l]), ffn1_w_gate[l], ffn1_w_up[l], ffn1_w_down[l])
        h = rmsnorm(x, mix_norm_g[l])
        proj = h @ w_in[l] + b_in[l]
        q, k, v, zg = jnp.split(proj, [ATTN_WIDTH, ATTN_WIDTH + KV_WIDTH,
                                       ATTN_WIDTH + 2 * KV_WIDTH], axis=-1)
        q = q.reshape(b, s, N_Q_HEADS, HEAD_DIM)
        k = k.reshape(b, s, N_KV_HEADS, HEAD_DIM)
        v = v.reshape(b, s, N_KV_HEADS, HEAD_DIM)
        y_attn = sliding_window_sink_attention(q, k, v, attn_sinks[l])
        y_gmlp = chunked_spatial_gating(jax.nn.gelu(zg), gmlp_ln_g[l], gmlp_ln_b[l],
                                        gmlp_w_s[l], gmlp_b_s[l])
        y = jnp.concatenate([rmsnorm(y_attn, attn_out_norm_g[l]),
                             rmsnorm(y_gmlp, gmlp_out_norm_g[l])], axis=-1)
        x = x + (y @ w_out[l] + b_out[l])
        x = x + FFN_RES * swiglu(rmsnorm(x, ffn2_norm_g[l]), ffn2_w_gate[l], ffn2_w_up[l], ffn2_w_down[l])
    return rmsnorm(x, final_norm_g)
```

```python
import contextlib
import numpy as np
import concourse.bass as bass
import concourse.mybir as mybir
from concourse.bass_utils import run_bass_kernel_spmd

F32 = mybir.dt.float32
BF16 = mybir.dt.bfloat16
AF = mybir.ActivationFunctionType
ALU = mybir.AluOpType

ENGS = ["pe", "act", "dve", "pool", "sp"]
EPS = 1e-6
NT = 2176
STW = 1152
NSLOT = 4
SLOTE = 5632
G1, GM, G2, GF, BA, BO, GA, GG, LG, LB, NCV = 0, 8, 16, 24, 32, 42, 50, 54, 58, 62, 66


class Buf:
    __slots__ = ("name", "writers", "readers")

    def __init__(self, name):
        self.name = name
        self.writers = []
        self.readers = []


class Slot:
    __slots__ = ("name", "count", "sem")

    def __init__(self, name):
        self.name = name
        self.count = 0
        self.sem = None


def _compact(lst):
    best = {}
    out = []
    for t in lst:
        if t[0] == "c":
            if t[1] not in best or best[t[1]][2] < t[2]:
                best[t[1]] = t
        else:
            out.append(t)
    return out + list(best.values())


class Prog:
    def __init__(self):
        self.ops = {e: [] for e in ENGS}
        self.slots = []

    def slot(self, name):
        s = Slot(name)
        self.slots.append(s)
        return s

    def op(self, eng, fn, reads=(), writes=(), join=(), dma=None):
        deps = []
        is_dma = dma is not None

        def add(tok, kind):
            if tok[0] == "c" and tok[1] == eng and not is_dma:
                if eng == "pe" or kind != "raw":
                    return
            deps.append(tok)

        for b in reads:
            for t in b.writers:
                add(t, "raw")
        for b in writes:
            for t in b.writers:
                add(t, "waw")
            for t in b.readers:
                add(t, "war")
        for b in join:
            for t in b.readers:
                add(t, "war")
        idx = len(self.ops[eng])
        if is_dma:
            dma.count += 1
            tok = ("d", dma, dma.count)
        else:
            tok = ("c", eng, idx)
        for b in reads:
            b.readers.append(tok)
            if len(b.readers) > 6:
                b.readers = _compact(b.readers)
        for b in writes:
            b.writers = [tok]
            b.readers = []
        for b in join:
            b.writers.append(tok)
        best = {}
        for t in deps:
            k = ("c", t[1]) if t[0] == "c" else ("d", id(t[1]))
            if k not in best or best[k][2] < t[2]:
                best[k] = t
        self.ops[eng].append({"fn": fn, "deps": list(best.values()), "signal": False, "dma": dma})
        return tok

    def emit(self, nc, final_waits=()):
        for e in ENGS:
            for o in self.ops[e]:
                for t in o["deps"]:
                    if t[0] == "c":
                        self.ops[t[1]][t[2]]["signal"] = True
        rank = {}
        for e in ENGS:
            r = 0
            rk = []
            for o in self.ops[e]:
                if o["signal"] and o["dma"] is None:
                    r += 1
                rk.append(r)
            rank[e] = rk
        with contextlib.ExitStack() as st:
            esem = {e: st.enter_context(nc.semaphore("s_" + e)) for e in ENGS}
            for s in self.slots:
                s.sem = st.enter_context(nc.semaphore("d_" + s.name))
            block = st.enter_context(nc.Block())

            def run(e, h):
                waited = {}
                for o in self.ops[e]:
                    for t in o["deps"]:
                        if t[0] == "c":
                            sem, val = esem[t[1]], rank[t[1]][t[2]]
                        else:
                            sem, val = t[1].sem, 16 * t[2]
                        k = id(sem)
                        if waited.get(k, 0) >= val:
                            continue
                        waited[k] = val
                        h.wait_ge(sem, val)
                    ins = o["fn"](h)
                    if o["dma"] is not None:
                        ins.then_inc(o["dma"].sem, 16)
                    elif o["signal"]:
                        ins.then_inc(esem[e], 1)
                if e == "sp":
                    for s in final_waits:
                        h.wait_ge(s.sem, 16 * s.count)

            @block.tensor
            def _(h):
                run("pe", h)

            @block.scalar
            def _(h):
                run("act", h)

            @block.vector
            def _(h):
                run("dve", h)

            @block.gpsimd
            def _(h):
                run("pool", h)

            @block.sync
            def _(h):
                run("sp", h)


def build_program(dbg=None):
    nc = bass.Bass("TRN2", target_bir_lowering=False)
    P = Prog()
    es = contextlib.ExitStack()

    def dram(name, shape, kind="ExternalInput"):
        return nc.dram_tensor(name, shape, F32, kind=kind).ap()

    xT_d = dram("xT", [1024, NT])
    wg_d = [dram("wg1", [1024, 2816]), dram("wg2", [1024, 2816])]
    wu_d = [dram("wu1", [1024, 2816]), dram("wu2", [1024, 2816])]
    wd_d = [dram("wd1", [2816, 1024]), dram("wd2", [2816, 1024])]
    winA_d = dram("winA", [1024, 1280])
    winB_d = dram("winB", [1024, 640])
    binB_d = dram("binB", [1, 640])
    woutL_d = dram("woutL", [128, 8, 1024])
    cvec_d = dram("cvec", [128, NCV])
    wsT_d = dram("wsT", [128, 8, 128])
    bsrep_d = dram("bsrep", [128, 4, 128])
    masks_d = dram("masks", [128, 3, 512])
    m01_d = dram("m01", [128, 512])
    sinks_d = dram("sinks4", [4, 2])
    sel_d = dram("sel", [128, 512])
    ident_d = dram("ident", [128, 128])
    outT_d = dram("outT", [1024, 2048], kind="ExternalOutput")

    def sb(name, shape, dt):
        return es.enter_context(nc.sbuf_tensor(name, shape, dt))

    x_sb = sb("x_sb", [128, 8, STW], F32)
    h_sb = sb("h_sb", [128, 8, STW], BF16)
    arena = sb("arena", [128, 34816], BF16)
    arena32 = arena.bitcast(F32)
    kT_sb = sb("kT_sb", [128, 2, NT], BF16)
    v_sb = sb("v_sb", [128, 17, 128], BF16)
    sq_sb = sb("sq_sb", [128, 3, 512], BF16)
    wst = sb("wst", [128, NSLOT, SLOTE], BF16)
    rs_sb = sb("rs_sb", [128, 2, 512], F32)
    zt_sb = sb("zt_sb", [128, 2, 512], F32)
    st_sb = sb("st_sb", [128, 1024], F32)
    ost_sb = zt_sb
    silu_sb = st_sb.bitcast(BF16)[:, 0:1536].rearrange("p (s t) -> p s t", s=3)
    st_sb = st_sb[:, :].rearrange("p (s t) -> p s t", s=2)
    cvec_sb = sb("cvec_sb", [128, NCV], F32)
    hb_sb = sb("hb_sb", [128, 20], F32)
    mask_sb = sb("mask_sb", [128, 3, 512], BF16)
    ws_sb = sb("ws_sb", [128, 8, 128], BF16)
    bp_sb = sb("bp_sb", [128, 4, 128], F32)
    ones_sb = sb("ones_sb", [128, 128], BF16)
    es_sb = sb("es_sb", [128, 2, 64], BF16)
    sk_sb = sb("sk_sb", [4, 2], F32)
    ek_sb = sb("ek_sb", [4, 2], F32)
    sel_sb = sb("sel_sb", [128, 512], BF16)
    ident_sb = sb("ident_sb", [128, 128], BF16)
    brow_sb = sb("brow_sb", [128, 640], BF16)
    eps_sb = sb("eps_sb", [128, 1], F32)
    lnst_sb = sb("lnst_sb", [128, 8, 6], F32)
    mv_sb = sb("mv_sb", [128, 8, 2], F32)
    lrs_sb = sb("lrs_sb", [128, 8], F32)
    dummy_sb = sb("dummy_sb", [128, 8], F32)
    ps = [es.enter_context(nc.psum_tensor(f"ps{i}", [128, 512], F32)) for i in range(8)]

    q3 = arena[:, 0:4096].rearrange("p (c t) -> p c t", c=4)
    u3 = arena[:, 4096:8192].rearrange("p (c t) -> p c t", c=4)
    gz3 = arena[:, 8192:12288].rearrange("p (j t) -> p j t", j=8)
    vn3 = arena[:, 12288:16384].rearrange("p (j t) -> p j t", j=8)
    yn3 = arena[:, 16384:20480].rearrange("p (k t) -> p k t", k=8)
    pt6 = arena[:, 20480:24576].rearrange("p (s k g r w) -> p s k g r w", s=2, k=2, g=2, r=2)
    pt5 = arena[:, 20480:24576].rearrange("p (s k g w) -> p s k g w", s=2, k=2, g=2)
    yr3 = arena32[:, 12288:16384].rearrange("p (k t) -> p k t", k=8)
    rd3 = arena32[:, 16384:17408].rearrange("p (s t) -> p s t", s=2)

    def actv(lf, t0, n):
        return arena[:, lf * STW + t0: lf * STW + t0 + n]

    XB = [[Buf(f"x{d}_{b}") for b in range(9)] for d in range(8)]
    HB = [[Buf(f"h{d}_{b}") for b in range(9)] for d in range(8)]
    AB = [[Buf(f"a{l}_{b}") for b in range(9)] for l in range(22)]
    PB = [Buf(f"pb{i}") for i in range(8)]
    WB = [Buf(f"wb{i}") for i in range(NSLOT)]
    WS = [P.slot(f"ws{i}") for i in range(NSLOT)]
    SQB = [Buf(f"sq{i}") for i in range(3)]
    RSB = [Buf(f"rs{i}") for i in range(2)]
    SIB = [Buf(f"si{i}") for i in range(3)]
    ZTB = [Buf(f"zt{i}") for i in range(2)]
    OSB = ZTB
    OS = [P.slot(f"os{i}") for i in range(2)]
    STB = [Buf(f"st{i}") for i in range(2)]
    XS = [P.slot(f"xs{i}") for i in range(3)]
    XSD = [P.slot(f"xsd{i}") for i in range(8)]
    CSL = {k: P.slot("c_" + k) for k in ["cvec", "bp", "sk", "mask", "ws", "sel", "brow", "ident", "m01"]}
    C = {k: Buf("c_" + k) for k in ["cvec", "mask", "ws", "bp", "es", "sel", "ones", "brow", "hb", "sk", "ek", "eps", "ident", "m01"]}
    QB = [[Buf(f"q{c}_{j}") for j in range(8)] for c in range(4)]
    UB = [[Buf(f"u{c}_{j}") for j in range(8)] for c in range(4)]
    GZB = [Buf(f"gz{j}") for j in range(8)]
    VNB = [Buf(f"vn{j}") for j in range(8)]
    KB = [[Buf(f"k{v}_{b}") for b in range(17)] for v in range(2)]
    VB = [Buf(f"v{b}") for b in range(17)]
    YRB = [[Buf(f"yr{k}_{b}") for b in range(4)] for k in range(8)]
    YNB = [Buf(f"yn{k}") for k in range(8)]
    PTB = [[Buf(f"pt{s}_{k}") for k in range(2)] for s in range(2)]
    RDB = [Buf(f"rd{i}") for i in range(2)]
    LNB = Buf("lnst")
    MVB = Buf("mv")
    LRB = Buf("lrs")
    FENCE = Buf("fence")
    DUM = Buf("dummy")

    rr = {}

    def nxt(key, n):
        v = rr.get(key, 0)
        rr[key] = (v + 1) % n
        return v

    def bank():
        i = nxt("bank", 8)
        return ps[i], PB[i]

    def blks(t0, n):
        return range(t0 // 128, (t0 + n) // 128)

    def mm(out, lhsT, rhs, start, stop):
        return lambda h: h.matmul(out, lhsT, rhs, start=start, stop=stop)

    def act_(out, in_, func, **kw):
        return lambda h: h.activation(out, in_, func, **kw)

    def stt(out, in0, scalar, in1, op0, op1):
        return lambda h: h.scalar_tensor_tensor(out, in0, scalar, in1, op0, op1)

    def tt(out, in0, in1, op):
        return lambda h: h.tensor_tensor(out, in0, in1, op)

    def ts(out, in0, s1, s2, op0, op1):
        return lambda h: h.tensor_scalar(out, in0, s1, s2, op0, op1)

    def dma(out, in_):
        return lambda h: h.dma_start(out=out, in_=in_)

    def fence_acquire():
        P.op("dve", lambda h: h.memset(dummy_sb[:, 0:1], 0.0), writes=[FENCE, DUM])

    def load_unit(pieces):
        def last_use(i):
            m = -1
            for t in WB[i].readers + WB[i].writers:
                if t[0] == "c" and t[1] == "pe":
                    m = max(m, t[2])
                elif t[0] == "d":
                    m = max(m, rr.get("slot_stamp%d" % i, -1))
            return m
        s = min(range(NSLOT), key=last_use)
        rr["slot_stamp%d" % s] = len(P.ops["pe"])
        first = True
        for (off, c, w, src) in pieces:
            dst = wst[:, s, off:off + c * w].rearrange("p (c w) -> p c w", c=c)
            P.op("pool", dma(dst, src), writes=[WB[s]] if first else [], join=[] if first else [WB[s]], dma=WS[s])
            first = False
        return s

    P.op("sp", dma(cvec_sb[:, :], cvec_d), writes=[C["cvec"]], dma=CSL["cvec"])
    P.op("sp", dma(bp_sb[:, :, :], bsrep_d), writes=[C["bp"]], dma=CSL["bp"])
    P.op("sp", dma(sk_sb[:, :], sinks_d), writes=[C["sk"]], dma=CSL["sk"])
    P.op("pool", dma(mask_sb[:, :, :], masks_d), writes=[C["mask"]], dma=CSL["mask"])
    P.op("pool", dma(ws_sb[:, :, :], wsT_d), writes=[C["ws"]], dma=CSL["ws"])
    P.op("pool", dma(sel_sb[:, :], sel_d), writes=[C["sel"]], dma=CSL["sel"])
    P.op("pool", dma(ident_sb[:, :], ident_d), writes=[C["ident"]], dma=CSL["ident"])
    xTv = xT_d.rearrange("(c p) t -> p c t", p=128)

    def load_x(st_, tiles):
        for i, (t0, n) in enumerate(tiles):
            c0 = t0 + st_ * 1024
            P.op("sp", dma(x_sb[:, :, t0:t0 + n], xTv[:, :, c0:c0 + n]),
                 writes=[XB[d][b] for d in range(8) for b in blks(t0, n)], dma=XS[i])

    P.op("dve", lambda h: h.memset(ones_sb[:, :], 1.0), writes=[C["ones"]])
    P.op("dve", lambda h: h.memset(eps_sb[:, :], EPS), writes=[C["eps"]])
    P.op("dve", lambda h: h.memset(brow_sb[:, :], 0.0), writes=[C["brow"]])
    P.op("pool", dma(brow_sb[0:1, :], binB_d), writes=[C["brow"]], dma=CSL["brow"])

    def late_setup():
        P.op("dve", lambda h: h.memset(es_sb[:, :, :], 0.0), writes=[C["es"]])
        P.op("dve", ts(hb_sb[:, 0:10], cvec_sb[:, BA:BA + 10], 0.5, None, ALU.mult, ALU.bypass),
             reads=[C["cvec"]], writes=[C["hb"]])
        P.op("dve", ts(hb_sb[:, 10:20], cvec_sb[:, BA:BA + 10], GS, None, ALU.mult, ALU.bypass),
             reads=[C["cvec"], C["hb"]], writes=[C["hb"]])
        P.op("pool", dma(sq_sb[:, 0, :], m01_d), writes=[SQB[0]], dma=CSL["m01"])
        for hlf in range(2):
            wv_ = ws_sb[:, 4 * hlf:4 * hlf + 4, :]
            P.op("dve", tt(wv_, wv_, sq_sb[:, 0, :].rearrange("p (g t) -> p g t", g=4), ALU.mult),
                 reads=[C["ws"], SQB[0]], writes=[C["ws"]])
        P.op("act", act_(ek_sb[:, :], sk_sb[:, :], AF.Exp), reads=[C["sk"]], writes=[C["ek"]])
        for g in range(2):
            P.op("dve", ts(es_sb[0:4, g, :], ones_sb[0:4, 0:64], ek_sb[0:4, g:g + 1], None, ALU.mult, ALU.bypass),
                 reads=[C["ek"], C["ones"], C["es"]], writes=[C["es"]])
        for g in range(8):
            c, gi = divmod(g, 2)
            pr, PR = bank()
            P.op("pe", mm(pr[:, 0:128], ones_sb[:, :], ws_sb[:, g, :], True, True),
                 reads=[C["ones"], C["ws"]], writes=[PR])
            sl = slice(gi * 64, (gi + 1) * 64)
            P.op("dve", stt(bp_sb[sl, c, :], pr[sl, 0:128], cvec_sb[sl, LB + c:LB + c + 1], bp_sb[sl, c, :],
                            ALU.mult, ALU.add),
                 reads=[PR, C["cvec"], C["bp"]], writes=[C["bp"]])

    def norm_stats(t0, n, src_rows, scale, use_dve=False):
        pq, PQ = bank()
        k = len(src_rows)
        for i, (ap, bufs) in enumerate(src_rows):
            qi = nxt("sq", 3)
            if use_dve and i % 2 == 1:
                P.op("dve", tt(sq_sb[:, qi, 0:n], ap, ap, ALU.mult), reads=bufs, writes=[SQB[qi]])
            else:
                P.op("act", act_(sq_sb[:, qi, 0:n], ap, AF.Square), reads=bufs, writes=[SQB[qi]])
            P.op("pe", mm(pq[:, 0:n], ones_sb[:, :], sq_sb[:, qi, 0:n], i == 0, i == k - 1),
                 reads=[SQB[qi], C["ones"]], writes=[PQ])
        ri = nxt("rs", 2)
        P.op("act", act_(rs_sb[:, ri, 0:n], pq[:, 0:n], AF.Ln, scale=scale, bias=eps_sb[:, 0:1]), reads=[PQ, C["eps"]],
             writes=[RSB[ri]])
        P.op("act", act_(rs_sb[:, ri, 0:n], rs_sb[:, ri, 0:n], AF.Exp, scale=-0.5), reads=[RSB[ri]],
             writes=[RSB[ri]])
        return ri

    def xrows(t0, n):
        return [(x_sb[:, d, t0:t0 + n], [XB[d][b] for b in blks(t0, n)]) for d in range(8)]

    def norm_to_h(gcol, tiles, use_dve=True):
        for (t0, n) in tiles:
            ri = norm_stats(t0, n, xrows(t0, n), 1.0 / 1024, use_dve)
            for d in range(8):
                P.op("dve", stt(h_sb[:, d, t0:t0 + n], x_sb[:, d, t0:t0 + n], cvec_sb[:, gcol + d:gcol + d + 1],
                                rs_sb[:, ri, 0:n], ALU.mult, ALU.mult),
                     reads=[XB[d][b] for b in blks(t0, n)] + [RSB[ri], C["cvec"]],
                     writes=[HB[d][b] for b in blks(t0, n)])

    h32 = h_sb.bitcast(F32)
    HALL = [HB[d][b] for d in range(8) for b in range(9)]
    OSH = [P.slot(f"osh{i}") for i in range(8)]

    def final_norm(st_, tiles, last=False):
        for (t0, n) in tiles:
            ri = norm_stats(t0, n, xrows(t0, n), 1.0 / 1024, True)
            for d in range(8):
                col0 = st_ * 1024 + (t0 - 128)
                if last:
                    stg = h32[:, d, 0:n]
                    P.op("dve", stt(stg, x_sb[:, d, t0:t0 + n], cvec_sb[:, GF + d:GF + d + 1],
                                    rs_sb[:, ri, 0:n], ALU.mult, ALU.mult),
                         reads=[XB[d][b] for b in blks(t0, n)] + [RSB[ri], C["cvec"]], writes=HALL if d == 0 else [],
                         join=[] if d == 0 else HALL)
                    P.op("sp", dma(outT_d[d * 128:(d + 1) * 128, col0:col0 + n], stg), reads=HALL, dma=OSH[d])
                    continue
                oi = nxt("os", 2)
                P.op("dve", stt(ost_sb[:, oi, 0:n], x_sb[:, d, t0:t0 + n], cvec_sb[:, GF + d:GF + d + 1],
                                rs_sb[:, ri, 0:n], ALU.mult, ALU.mult),
                     reads=[XB[d][b] for b in blks(t0, n)] + [RSB[ri], C["cvec"]], writes=[OSB[oi]])
                P.op("sp", dma(outT_d[d * 128:(d + 1) * 128, col0:col0 + n], ost_sb[:, oi, 0:n]),
                     reads=[OSB[oi]], dma=OS[oi])

    def ffn_A(fi, tiles, hooks=None, lead=NSLOT):
        hooks = hooks or {}
        wgv = wg_d[fi].rearrange("(c p) f -> p c f", p=128)
        wuv = wu_d[fi].rearrange("(c p) f -> p c f", p=128)

        def unit_load(u):
            return load_unit([(0, 8, 256, wgv[:, :, u * 256:(u + 1) * 256]),
                              (2048, 8, 256, wuv[:, :, u * 256:(u + 1) * 256])])

        def unit_compute(u, s, tl):
            wgs = wst[:, s, 0:2048].rearrange("p (c w) -> p c w", c=8)
            wus = wst[:, s, 2048:4096].rearrange("p (c w) -> p c w", c=8)
            for k in range(2):
                lf = 2 * u + k
                for (t0, n) in tl:
                    bl = list(blks(t0, n))
                    pg, PG = bank()
                    pu, PU = bank()
                    for d in range(8):
                        P.op("pe", mm(pg[:, 0:n], wgs[:, d, k * 128:(k + 1) * 128], h_sb[:, d, t0:t0 + n], d == 0, d == 7),
                             reads=[WB[s]] + [HB[d][b] for b in bl], writes=[PG])
                    for d in range(8):
                        P.op("pe", mm(pu[:, 0:n], wus[:, d, k * 128:(k + 1) * 128], h_sb[:, d, t0:t0 + n], d == 0, d == 7),
                             reads=[WB[s]] + [HB[d][b] for b in bl], writes=[PU])
                    si = nxt("si", 3)
                    P.op("act", act_(silu_sb[:, si, 0:n], pg[:, 0:n], AF.Silu), reads=[PG],
                         writes=[SIB[si], STB[si // 2]])
                    P.op("dve", tt(actv(lf, t0, n), silu_sb[:, si, 0:n], pu[:, 0:n], ALU.mult),
                         reads=[SIB[si], STB[si // 2], PU, FENCE], writes=[AB[lf][b] for b in bl])

        head = list(range(lead))
        slots = {u: unit_load(u) for u in head}
        for ti, tile in enumerate(tiles):
            for u in head:
                unit_compute(u, slots[u], [tile])
                if (ti, u) in hooks:
                    hooks[(ti, u)]()
        for u in range(lead, 11):
            unit_compute(u, unit_load(u), tiles)

    def ffn_B(fi, tiles, post, post_late=None, split=3):
        wdv = wd_d[fi].rearrange("(c p) m -> p c m", p=128)

        def load(dg):
            return load_unit([(0, 22, 256, wdv[:, :, dg * 256:(dg + 1) * 256])])

        def compute(dg, s, tl, mid_hook=None):
            wds = wst[:, s, 0:5632].rearrange("p (c w) -> p c w", c=22)
            for k in range(2):
                if k == 1 and mid_hook is not None:
                    mid_hook()
                dm = 2 * dg + k
                for (t0, n) in tl:
                    bl = list(blks(t0, n))
                    po, PO = bank()
                    for lf in range(22):
                        P.op("pe", mm(po[:, 0:n], wds[:, lf, k * 128:(k + 1) * 128], actv(lf, t0, n), lf == 0, lf == 21),
                             reads=[WB[s], FENCE] + [AB[lf][b] for b in bl], writes=[PO])
                    xb_ = [XB[dm][b] for b in bl]
                    P.op("dve", stt(x_sb[:, dm, t0:t0 + n], po[:, 0:n], 0.5, x_sb[:, dm, t0:t0 + n], ALU.mult, ALU.add),
                         reads=[PO] + xb_, writes=xb_)

        for dg in range(4 - split):
            compute(dg, load(dg), tiles)
        last = list(range(4 - split, 4))
        slots = {dg: load(dg) for dg in last}
        pending = None
        for ti, tile in enumerate(tiles):
            for i, dg in enumerate(last):
                if pending is not None and i == 0:
                    compute(dg, slots[dg], [tile], mid_hook=lambda p_=pending: post(p_))
                else:
                    compute(dg, slots[dg], [tile])
                if pending is not None and i == len(last) - 1 and post_late is not None:
                    post_late(pending)
            pending = ti
        return pending

    GS = (2 * 0.044715) ** 0.5

    def gelu(dst, src, n, sbias, hbias, src_bufs, extra_reads, dst_bufs):
        gi = nxt("gt", 2)
        zt = zt_sb[:, gi, 0:n]
        sv = st_sb[:, gi, 0:n]
        if sbias is None:
            P.op("act", act_(zt, src, AF.Identity, scale=0.5), reads=src_bufs, writes=[ZTB[gi]])
            P.op("act", act_(sv, src, AF.Square, scale=GS), reads=src_bufs, writes=[STB[gi]])
        else:
            P.op("act", act_(zt, src, AF.Identity, scale=0.5, bias=hbias), reads=src_bufs + [C["hb"]], writes=[ZTB[gi]])
            P.op("act", act_(sv, src, AF.Square, scale=GS, bias=sbias), reads=src_bufs + [C["hb"]], writes=[STB[gi]])
        P.op("dve", stt(sv, sv, 2.0, zt, ALU.add, ALU.mult), reads=[STB[gi], ZTB[gi]], writes=[STB[gi]])
        P.op("act", act_(sv, sv, AF.Tanh, scale=0.7978845608028654), reads=[STB[gi]], writes=[STB[gi]])
        P.op("dve", stt(dst, sv, 1.0, zt, ALU.add, ALU.mult), reads=[STB[gi], ZTB[gi]] + extra_reads, writes=dst_bufs)

    def mixer(st_, tiles_all, own, pre=None):
        fence_acquire()
        winAv = winA_d.rearrange("(c p) f -> p c f", p=128)
        winBv = winB_d.rearrange("(c p) f -> p c f", p=128)
        state = {"pre": pre}

        def loadA(u):
            return load_unit([(0, 8, 256, winAv[:, :, u * 256:(u + 1) * 256])])

        def projA(ci, s, k, tl):
            wa = wst[:, s, 0:2048].rearrange("p (c w) -> p c w", c=8)
            for (t0, n) in tl:
                bl = list(blks(t0, n))
                pb, PBk = bank()
                for d in range(8):
                    P.op("pe", mm(pb[:, 0:n], wa[:, d, k * 128:(k + 1) * 128], h_sb[:, d, t0:t0 + n], d == 0, d == 7),
                         reads=[WB[s]] + [HB[d][b] for b in bl], writes=[PBk])
                if state["pre"] is not None:
                    state["pre"]()
                    state["pre"] = None
                bcol = cvec_sb[:, BA + ci:BA + ci + 1]
                if ci < 4:
                    k0 = t0 - 128
                    P.op("act", act_(q3[:, ci, k0:k0 + n], pb[:, 0:n], AF.Identity, bias=bcol),
                         reads=[PBk, C["cvec"], FENCE], writes=[QB[ci][j] for j in blks(k0, n)])
                elif ci < 6:
                    var = ci - 4
                    gc = t0 + st_ * 1024
                    P.op("act", act_(kT_sb[:, var, gc:gc + n], pb[:, 0:n], AF.Identity, bias=bcol),
                         reads=[PBk, C["cvec"]], writes=[KB[var][b] for b in blks(gc, n)])
                else:
                    c = ci - 6
                    k0 = t0 - 128
                    gelu(u3[:, c, k0:k0 + n], pb[:, 0:n], n, hb_sb[:, 10 + ci:11 + ci], hb_sb[:, ci:ci + 1], [PBk], [FENCE],
                         [UB[c][j] for j in blks(k0, n)])

        for u in range(3):
            s = loadA(u)
            for k in range(2):
                ci = 2 * u + k
                projA(ci, s, k, tiles_all if ci in (4, 5) else own)
        s_v = load_unit([(0, 8, 128, winBv[:, :, 512:640])])
        wv = wst[:, s_v, 0:1024].rearrange("p (c w) -> p c w", c=8)
        xblocks = sorted({b for (t0, n) in tiles_all for b in blks(t0, n)})
        for xb in xblocks:
            gb = xb + 8 * st_
            tsl = slice(xb * 128, (xb + 1) * 128)
            pv_, PV = bank()
            for d in range(8):
                P.op("pe", mm(pv_[:, 0:128], h_sb[:, d, tsl], wv[:, d, 0:128], d == 0, False),
                     reads=[WB[s_v], HB[d][xb]], writes=[PV])
            P.op("pe", mm(pv_[:, 0:128], ones_sb[:, 0:128], brow_sb[:, 512:640], False, True),
                 reads=[C["ones"], C["brow"]], writes=[PV])
            P.op("act", act_(v_sb[:, gb, :], pv_[:, 0:128], AF.Copy), reads=[PV], writes=[VB[gb]])
        s_u = [loadA(3), loadA(4)]
        s_zv = load_unit([(0, 8, 512, winBv[:, :, 0:512])])
        wz = wst[:, s_zv, 0:4096].rearrange("p (c w) -> p c w", c=8)

        def zv_block(j):
            xb = j + 1
            tsl = slice(xb * 128, (xb + 1) * 128)
            pz, PZ = bank()
            for d in range(8):
                P.op("pe", mm(pz[:, 0:512], h_sb[:, d, tsl], wz[:, d, 0:512], d == 0, False),
                     reads=[WB[s_zv], HB[d][xb]], writes=[PZ])
            P.op("pe", mm(pz[:, 0:512], ones_sb[:, 0:128], brow_sb[:, 0:512], False, True),
                 reads=[C["ones"], C["brow"]], writes=[PZ])
            gelu(gz3[:, j, :], pz[:, 0:512], 512, None, None, [PZ], [FENCE], [GZB[j]])
            P.op("dve", lambda h, j=j: h.bn_stats(lnst_sb[:, j, :], gz3[:, j, :]),
                 reads=[GZB[j], FENCE], join=[LNB])
            P.op("dve", lambda h, j=j: h.bn_aggr(mv_sb[:, j, :], lnst_sb[:, j, :]), reads=[LNB], join=[MVB])

        def ln_finish(j0):
            P.op("act", act_(lrs_sb[:, j0:j0 + 4], mv_sb[:, j0:j0 + 4, 1], AF.Ln, bias=eps_sb[:, 0:1]),
                 reads=[MVB, C["eps"]], join=[LRB])
            P.op("act", act_(lrs_sb[:, j0:j0 + 4], lrs_sb[:, j0:j0 + 4], AF.Exp, scale=-0.5), reads=[LRB], join=[LRB])
            for j in range(j0, j0 + 4):
                P.op("dve", ts(vn3[:, j, :], gz3[:, j, :], mv_sb[:, j, 0:1], lrs_sb[:, j:j + 1], ALU.subtract, ALU.mult),
                     reads=[GZB[j], MVB, LRB, FENCE], writes=[VNB[j]])

        def scores(j):
            gb = j + 1 + 8 * st_
            k0 = j * 128
            pset = j % 2
            for kb in range(2):
                kgb = gb - 1 + kb
                mi = 0 if kb == 1 else (2 if kgb == 0 else 1)
                for par in range(2):
                    psl = slice(par * 64, (par + 1) * 64)
                    pc, PC = bank()
                    P.op("pe", mm(pc[:, 0:512], ident_sb[:, :], mask_sb[:, mi, :], True, False),
                         reads=[C["ident"], C["mask"]], writes=[PC])
                    for g in range(2):
                        var = 0 if g == par else 1
                        P.op("pe", mm(pc[:, g * 256:(g + 1) * 256], kT_sb[psl, var, kgb * 128:(kgb + 1) * 128],
                                      q3[psl, 2 * g:2 * g + 2, k0:k0 + 128], False, g == 1),
                             reads=[KB[var][kgb], QB[2 * g][j], QB[2 * g + 1][j], FENCE], writes=[PC])
                    P.op("act", act_(pt6[:, pset, kb, :, par, :], pc[:, 0:512].rearrange("p (g w) -> p g w", g=2),
                                     AF.Exp, scale=0.125),
                         reads=[PC, FENCE], writes=[PTB[pset][kb]])

        def gmlp(j):
            bi = j % 4
            k0 = j * 128
            bsl = slice(bi * 128, (bi + 1) * 128)
            pm, PM = bank()
            pm3 = pm[:, 0:512].rearrange("p (c t) -> p c t", c=4)
            for c in range(4):
                for gi in range(2):
                    g = 2 * c + gi
                    P.op("pe", mm(pm3[gi * 64:(gi + 1) * 64, c, :], vn3[:, j, g * 64:(g + 1) * 64], ws_sb[:, g, :], True, True),
                         reads=[VNB[j], C["ws"], FENCE], writes=[PM])
            for c in range(4):
                P.op("dve", stt(yr3[:, 4 + c, bsl], pm3[:, c, :], cvec_sb[:, LG + c:LG + c + 1], bp_sb[:, c, :],
                                ALU.mult, ALU.add),
                     reads=[PM, C["cvec"], C["bp"], FENCE], writes=[YRB[4 + c][bi]])
            P.op("dve", tt(yr3[:, 4:8, bsl], yr3[:, 4:8, bsl], u3[:, :, k0:k0 + 128], ALU.mult),
                 reads=[YRB[4 + c][bi] for c in range(4)] + [UB[c][j] for c in range(4)],
                 writes=[YRB[4 + c][bi] for c in range(4)])

        def pv(j):
            gb = j + 1 + 8 * st_
            pset = j % 2
            bi = j % 4
            bsl = slice(bi * 128, (bi + 1) * 128)
            po, PO = bank()
            pd, PD = bank()
            for g in range(2):
                gsl = slice(g * 64, (g + 1) * 64)
                for kb in range(2):
                    kgb = gb - 1 + kb
                    P.op("pe", mm(po[gsl, 0:512], v_sb[:, kgb, gsl], pt5[:, pset, kb, g, :], kb == 0, kb == 1),
                         reads=[VB[kgb], PTB[pset][kb], FENCE], writes=[PO])
                for kb in range(2):
                    P.op("pe", mm(pd[gsl, 0:512], ones_sb[:, 0:64], pt5[:, pset, kb, g, :], kb == 0, False),
                         reads=[C["ones"], PTB[pset][kb], FENCE], writes=[PD])
                P.op("pe", mm(pd[gsl, 0:512], es_sb[:, g, :], sel_sb[:, :], False, True),
                     reads=[C["es"], C["sel"]], writes=[PD])
            ri2 = j % 2
            P.op("dve", lambda h: h.reciprocal(rd3[:, ri2, :], pd[:, 0:512]), reads=[PD, FENCE], writes=[RDB[ri2]])
            P.op("dve", tt(yr3[:, 0:4, bsl], po[:, 0:512].rearrange("p (r t) -> p r t", r=4),
                           rd3[:, ri2, :].rearrange("p (r t) -> p r t", r=4), ALU.mult),
                 reads=[PO, RDB[ri2], FENCE], writes=[YRB[r][bi] for r in range(4)])

        def ynorm():
            ris = []
            for half in range(2):
                rows = [(yr3[:, half * 4 + r, :], [YRB[half * 4 + r][b] for b in range(4)] + [FENCE]) for r in range(4)]
                ris.append(norm_stats(0, 512, rows, 1.0 / 512))
            for half in range(2):
                for r in range(4):
                    kk = half * 4 + r
                    gcol = (GA if half == 0 else GG) + r
                    P.op("dve", stt(yn3[:, kk, :], yr3[:, kk, :], cvec_sb[:, gcol:gcol + 1], rs_sb[:, ris[half], :],
                                    ALU.mult, ALU.mult),
                         reads=[YRB[kk][b] for b in range(4)] + [RSB[ris[half]], C["cvec"], FENCE], writes=[YNB[kk]])

        wo_slots = {}

        def wout_group(ti, dm):
            dh, k = divmod(dm, 4)
            if dh not in wo_slots:
                wo_slots[dh] = load_unit([(0, 8, 512, woutL_d[:, :, dh * 512:(dh + 1) * 512])])
            s = wo_slots[dh]
            wo = wst[:, s, 0:4096].rearrange("p (c w) -> p c w", c=8)
            t0x = 128 + ti * 512
            bl = list(blks(t0x, 512))
            po2, PO2 = bank()
            for kk in range(8):
                P.op("pe", mm(po2[:, 0:512], wo[:, kk, k * 128:(k + 1) * 128], yn3[:, kk, :], kk == 0, kk == 7),
                     reads=[WB[s], YNB[kk], FENCE], writes=[PO2])
            xb_ = [XB[dm][b] for b in bl]
            P.op("dve", stt(x_sb[:, dm, t0x:t0x + 512], po2[:, 0:512], cvec_sb[:, BO + dm:BO + dm + 1],
                            x_sb[:, dm, t0x:t0x + 512], ALU.add, ALU.add),
                 reads=[PO2, C["cvec"]] + xb_, writes=xb_)

        scores(0)
        for j in range(8):
            if j + 1 < 8:
                scores(j + 1)
            if j < 4:
                c = j
                projA(6 + c, s_u[c // 2], c % 2, own)
                zv_block(2 * j)
                zv_block(2 * j + 1)
            else:
                gmlp(j)
                wout_group(0, 2 * (j - 4))
                wout_group(0, 2 * (j - 4) + 1)
            pv(j)
            if j == 3:
                ln_finish(0)
                ln_finish(4)
                for jj in range(4):
                    gmlp(jj)
                ynorm()
        norm_to_h(G2, [own[0]], use_dve=False)
        ynorm()
        for dm in range(8):
            wout_group(1, dm)
        norm_to_h(G2, [own[1]], use_dve=False)

    own = [(128, 512), (640, 512)]
    t_all0 = [(0, 384), (384, 384), (768, 384)]
    finals = list(OS) + list(OSH)
    for d in range(8):
        P.op("sp", dma(x_sb[:, d, 0:384], xT_d[d * 128:(d + 1) * 128, 0:384]),
             writes=[XB[d][b] for b in blks(0, 384)], dma=XSD[d])
    for i in (1, 2):
        t0, n = t_all0[i]
        P.op("sp", dma(x_sb[:, :, t0:t0 + n], xTv[:, :, t0:t0 + n]),
             writes=[XB[d][b] for d in range(8) for b in blks(t0, n)], dma=XS[i])
    norm_to_h(G1, t_all0)
    hooks = {}
    for st_ in range(2):
        tiles_all = t_all0 if st_ == 0 else own
        ffn_A(0, tiles_all, hooks)
        if st_ == 0:
            late_setup()
        pend = ffn_B(0, tiles_all, post=lambda ti: norm_to_h(GM, [tiles_all[ti]]))
        mixer(st_, tiles_all, own, pre=lambda: norm_to_h(GM, [tiles_all[pend]]))
        fence_acquire()
        ffn_A(1, own)

        def postA(ti, st_=st_):
            final_norm(st_, [own[ti]], last=(st_ == 1 and ti == 1))
            if st_ == 0:
                t0, n = own[ti]
                for d in range(8):
                    P.op("sp", dma(x_sb[:, d, t0:t0 + n], xT_d[d * 128:(d + 1) * 128, 1024 + t0:1024 + t0 + n]),
                         writes=[XB[d][b] for b in blks(t0, n)], dma=XSD[d])

        def postB(ti, st_=st_):
            if st_ == 0:
                norm_to_h(G1, [own[ti]])

        pend2 = ffn_B(1, own, post=postA, post_late=postB)
        if st_ == 0:
            hooks = {(0, 0): (lambda: postA(pend2)), (0, 2): (lambda: postB(pend2))}
        else:
            postA(pend2)
    P.emit(nc, final_waits=finals)
    return nc


def make_in_maps(inp):
    f = lambda k: np.asarray(inp[k], dtype=np.float32)
    x = f("x")
    w_in = f("w_in")[0]
    b_in = f("b_in")[0]
    w_out = f("w_out")[0]
    qcols = []
    for g in range(2):
        for i in range(2):
            h0, h1 = 4 * g + i, 4 * g + 2 + i
            qcols += list(range(h0 * 64, h0 * 64 + 64)) + list(range(h1 * 64, h1 * 64 + 64))
    colsA = qcols + list(range(512, 640)) + list(range(576, 640)) + list(range(512, 576)) + list(range(768, 1280))
    colsB = list(range(1280, 1792)) + list(range(640, 768))
    winA = np.ascontiguousarray(w_in[:, colsA])
    winB = np.ascontiguousarray(w_in[:, colsB])
    binA = b_in[colsA]
    binB = np.ascontiguousarray(b_in[colsB][None, :])
    pc = lambda v, n: np.ascontiguousarray(v.reshape(n, 128).T)
    attn_g = f("attn_out_norm_g")[0]
    cvec = np.concatenate([
        pc(f("ffn1_norm_g")[0], 8), pc(f("mix_norm_g")[0], 8), pc(f("ffn2_norm_g")[0], 8), pc(f("final_norm_g"), 8),
        pc(binA, 10), pc(f("b_out")[0], 8),
        np.ascontiguousarray(attn_g.reshape(2, 4, 64).transpose(0, 2, 1).reshape(128, 4)),
        pc(f("gmlp_out_norm_g")[0], 4), pc(f("gmlp_ln_g")[0], 4), pc(f("gmlp_ln_b")[0], 4)], axis=1)
    cvec = np.ascontiguousarray(cvec.astype(np.float32))
    assert cvec.shape == (128, NCV)
    woA = w_out[:512].reshape(2, 4, 64, 1024).transpose(0, 2, 1, 3).reshape(128, 4, 1024)
    woG = w_out[512:].reshape(4, 128, 1024).transpose(1, 0, 2)
    woutL = np.ascontiguousarray(np.concatenate([woA, woG], axis=1))
    wsT = np.ascontiguousarray(f("gmlp_w_s")[0].transpose(2, 0, 1))
    b_s = f("gmlp_b_s")[0]
    bsrep = np.ascontiguousarray(np.repeat(b_s, 64, axis=0).reshape(4, 128, 128).transpose(1, 0, 2))
    kk = np.arange(128)[:, None]
    qq = np.arange(128)[None, :]
    NEG = np.float32(-30000.0)
    m01 = np.tile((qq >= kk).astype(np.float32), (1, 4))
    mC = np.tile(np.where(qq >= kk, np.float32(0.0), NEG).astype(np.float32), (1, 4))
    mP = np.tile(np.where(kk > qq, np.float32(0.0), NEG).astype(np.float32), (1, 4))
    sinks4 = np.ascontiguousarray(f("attn_sinks")[0].reshape(2, 4).T)
    sel = np.zeros((128, 512), np.float32)
    for r in range(4):
        sel[r, r * 128:(r + 1) * 128] = 1.0
    common = {
        "wg1": f("ffn1_w_gate")[0], "wu1": f("ffn1_w_up")[0], "wd1": f("ffn1_w_down")[0],
        "wg2": f("ffn2_w_gate")[0], "wu2": f("ffn2_w_up")[0], "wd2": f("ffn2_w_down")[0],
        "winA": winA, "winB": winB, "binB": binB, "woutL": woutL, "cvec": cvec, "wsT": wsT, "bsrep": bsrep,
        "sinks4": sinks4, "sel": sel, "ident": np.eye(128, dtype=np.float32), "m01": m01,
    }
    common = {k: np.ascontiguousarray(v, dtype=np.float32) for k, v in common.items()}
    in_maps = []
    for c in range(8):
        b, qd = divmod(c, 4)
        s0 = qd * 2048
        xs = np.zeros((NT, 1024), np.float32)
        xs[128:] = x[b, s0:s0 + 2048]
        if qd > 0:
            xs[:128] = x[b, s0 - 128:s0]
        m = dict(common)
        m["xT"] = np.ascontiguousarray(xs.T)
        m["masks"] = np.ascontiguousarray(np.stack([mC, mP, mP if qd > 0 else np.full_like(mP, NEG)], axis=1))
        in_maps.append(m)
    return in_maps


def kernel(**inputs):
    in_maps = make_in_maps(inputs)
    nc = build_program()
    res = run_bass_kernel_spmd(nc, in_maps, core_ids=list(range(8)))
    out = np.empty((2, 8192, 1024), np.float32)
    for c in range(8):
        b, qd = divmod(c, 4)
        out[b, qd * 2048:(qd + 1) * 2048, :] = np.asarray(res.results[c]["outT"], dtype=np.float32).T
    return out
```

```python
import contextlib
import numpy as np
import concourse.bass as bass
import concourse.mybir as mybir
from concourse.bass_utils import run_bass_kernel_spmd

F32 = mybir.dt.float32
BF16 = mybir.dt.bfloat16
AF = mybir.ActivationFunctionType
ALU = mybir.AluOpType

ENGS = ["pe", "act", "dve", "pool", "sp"]
EPS = 1e-6
NT = 2176
STW = 1152
NSLOT = 4
SLOTE = 5632
G1, GM, G2, GF, BA, BO, GA, GG, LG, LB, NCV = 0, 8, 16, 24, 32, 42, 50, 54, 58, 62, 66


class Buf:
    __slots__ = ("name", "writers", "readers")

    def __init__(self, name):
        self.name = name
        self.writers = []
        self.readers = []


class Slot:
    __slots__ = ("name", "count", "sem")

    def __init__(self, name):
        self.name = name
        self.count = 0
        self.sem = None


def _compact(lst):
    best = {}
    out = []
    for t in lst:
        if t[0] == "c":
            if t[1] not in best or best[t[1]][2] < t[2]:
                best[t[1]] = t
        else:
            out.append(t)
    return out + list(best.values())


class Prog:
    def __init__(self):
        self.ops = {e: [] for e in ENGS}
        self.slots = []

    def slot(self, name):
        s = Slot(name)
        self.slots.append(s)
        return s

    def op(self, eng, fn, reads=(), writes=(), join=(), dma=None):
        deps = []
        is_dma = dma is not None

        def add(tok, kind):
            if tok[0] == "c" and tok[1] == eng and not is_dma:
                if eng == "pe" or kind != "raw":
                    return
            deps.append(tok)

        for b in reads:
            for t in b.writers:
                add(t, "raw")
        for b in writes:
            for t in b.writers:
                add(t, "waw")
            for t in b.readers:
                add(t, "war")
        for b in join:
            for t in b.readers:
                add(t, "war")
        idx = len(self.ops[eng])
        if is_dma:
            dma.count += 1
            tok = ("d", dma, dma.count)
        else:
            tok = ("c", eng, idx)
        for b in reads:
            b.readers.append(tok)
            if len(b.readers) > 6:
                b.readers = _compact(b.readers)
        for b in writes:
            b.writers = [tok]
            b.readers = []
        for b in join:
            b.writers.append(tok)
        best = {}
        for t in deps:
            k = ("c", t[1]) if t[0] == "c" else ("d", id(t[1]))
            if k not in best or best[k][2] < t[2]:
                best[k] = t
        self.ops[eng].append({"fn": fn, "deps": list(best.values()), "signal": False, "dma": dma})
        return tok

    def emit(self, nc, final_waits=()):
        for e in ENGS:
            for o in self.ops[e]:
                for t in o["deps"]:
                    if t[0] == "c":
                        self.ops[t[1]][t[2]]["signal"] = True
        rank = {}
        for e in ENGS:
            r = 0
            rk = []
            for o in self.ops[e]:
                if o["signal"] and o["dma"] is None:
                    r += 1
                rk.append(r)
            rank[e] = rk
        with contextlib.ExitStack() as st:
            esem = {e: st.enter_context(nc.semaphore("s_" + e)) for e in ENGS}
            for s in self.slots:
                s.sem = st.enter_context(nc.semaphore("d_" + s.name))
            block = st.enter_context(nc.Block())

            def run(e, h):
                waited = {}
                for o in self.ops[e]:
                    for t in o["deps"]:
                        if t[0] == "c":
                            sem, val = esem[t[1]], rank[t[1]][t[2]]
                        else:
                            sem, val = t[1].sem, 16 * t[2]
                        k = id(sem)
                        if waited.get(k, 0) >= val:
                            continue
                        waited[k] = val
                        h.wait_ge(sem, val)
                    ins = o["fn"](h)
                    if o["dma"] is not None:
                        ins.then_inc(o["dma"].sem, 16)
                    elif o["signal"]:
                        ins.then_inc(esem[e], 1)
                if e == "sp":
                    for s in final_waits:
                        h.wait_ge(s.sem, 16 * s.count)

            @block.tensor
            def _(h):
                run("pe", h)

            @block.scalar
            def _(h):
                run("act", h)

            @block.vector
            def _(h):
                run("dve", h)

            @block.gpsimd
            def _(h):
                run("pool", h)

            @block.sync
            def _(h):
                run("sp", h)


def build_program(dbg=None):
    nc = bass.Bass("TRN2", target_bir_lowering=False)
    P = Prog()
    es = contextlib.ExitStack()

    def dram(name, shape, kind="ExternalInput"):
        return nc.dram_tensor(name, shape, F32, kind=kind).ap()

    xT_d = dram("xT", [1024, NT])
    wg_d = [dram("wg1", [1024, 2816]), dram("wg2", [1024, 2816])]
    wu_d = [dram("wu1", [1024, 2816]), dram("wu2", [1024, 2816])]
    wd_d = [dram("wd1", [2816, 1024]), dram("wd2", [2816, 1024])]
    winA_d = dram("winA", [1024, 1280])
    winB_d = dram("winB", [1024, 640])
    binB_d = dram("binB", [1, 640])
    woutL_d = dram("woutL", [128, 8, 1024])
    cvec_d = dram("cvec", [128, NCV])
    wsT_d = dram("wsT", [128, 8, 128])
    bsrep_d = dram("bsrep", [128, 4, 128])
    masks_d = dram("masks", [128, 3, 512])
    m01_d = dram("m01", [128, 512])
    sinks_d = dram("sinks4", [4, 2])
    sel_d = dram("sel", [128, 512])
    ident_d = dram("ident", [128, 128])
    outT_d = dram("outT", [1024, 2048], kind="ExternalOutput")

    def sb(name, shape, dt):
        return es.enter_context(nc.sbuf_tensor(name, shape, dt))

    x_sb = sb("x_sb", [128, 8, STW], F32)
    h_sb = sb("h_sb", [128, 8, STW], BF16)
    arena = sb("arena", [128, 34816], BF16)
    arena32 = arena.bitcast(F32)
    kT_sb = sb("kT_sb", [128, 2, NT], BF16)
    v_sb = sb("v_sb", [128, 17, 128], BF16)
    sq_sb = sb("sq_sb", [128, 3, 512], BF16)
    wst = sb("wst", [128, NSLOT, SLOTE], BF16)
    rs_sb = sb("rs_sb", [128, 2, 512], F32)
    zt_sb = sb("zt_sb", [128, 2, 512], F32)
    st_sb = sb("st_sb", [128, 1024], F32)
    ost_sb = zt_sb
    silu_sb = st_sb.bitcast(BF16)[:, 0:1536].rearrange("p (s t) -> p s t", s=3)
    st_sb = st_sb[:, :].rearrange("p (s t) -> p s t", s=2)
    cvec_sb = sb("cvec_sb", [128, NCV], F32)
    hb_sb = sb("hb_sb", [128, 20], F32)
    mask_sb = sb("mask_sb", [128, 3, 512], BF16)
    ws_sb = sb("ws_sb", [128, 8, 128], BF16)
    bp_sb = sb("bp_sb", [128, 4, 128], F32)
    ones_sb = sb("ones_sb", [128, 128], BF16)
    es_sb = sb("es_sb", [128, 2, 64], BF16)
    sk_sb = sb("sk_sb", [4, 2], F32)
    ek_sb = sb("ek_sb", [4, 2], F32)
    sel_sb = sb("sel_sb", [128, 512], BF16)
    ident_sb = sb("ident_sb", [128, 128], BF16)
    brow_sb = sb("brow_sb", [128, 640], BF16)
    eps_sb = sb("eps_sb", [128, 1], F32)
    lnst_sb = sb("lnst_sb", [128, 8, 6], F32)
    mv_sb = sb("mv_sb", [128, 8, 2], F32)
    lrs_sb = sb("lrs_sb", [128, 8], F32)
    dummy_sb = sb("dummy_sb", [128, 8], F32)
    ps = [es.enter_context(nc.psum_tensor(f"ps{i}", [128, 512], F32)) for i in range(8)]

    q3 = arena[:, 0:4096].rearrange("p (c t) -> p c t", c=4)
    u3 = arena[:, 4096:8192].rearrange("p (c t) -> p c t", c=4)
    gz3 = arena[:, 8192:12288].rearrange("p (j t) -> p j t", j=8)
    vn3 = arena[:, 12288:16384].rearrange("p (j t) -> p j t", j=8)
    yn3 = arena[:, 16384:20480].rearrange("p (k t) -> p k t", k=8)
    pt6 = arena[:, 20480:24576].rearrange("p (s k g r w) -> p s k g r w", s=2, k=2, g=2, r=2)
    pt5 = arena[:, 20480:24576].rearrange("p (s k g w) -> p s k g w", s=2, k=2, g=2)
    yr3 = arena32[:, 12288:16384].rearrange("p (k t) -> p k t", k=8)
    rd3 = arena32[:, 16384:17408].rearrange("p (s t) -> p s t", s=2)

    def actv(lf, t0, n):
        return arena[:, lf * STW + t0: lf * STW + t0 + n]

    XB = [[Buf(f"x{d}_{b}") for b in range(9)] for d in range(8)]
    HB = [[Buf(f"h{d}_{b}") for b in range(9)] for d in range(8)]
    AB = [[Buf(f"a{l}_{b}") for b in range(9)] for l in range(22)]
    PB = [Buf(f"pb{i}") for i in range(8)]
    WB = [Buf(f"wb{i}") for i in range(NSLOT)]
    WS = [P.slot(f"ws{i}") for i in range(NSLOT)]
    SQB = [Buf(f"sq{i}") for i in range(3)]
    RSB = [Buf(f"rs{i}") for i in range(2)]
    SIB = [Buf(f"si{i}") for i in range(3)]
    ZTB = [Buf(f"zt{i}") for i in range(2)]
    OSB = ZTB
    OS = [P.slot(f"os{i}") for i in range(2)]
    STB = [Buf(f"st{i}") for i in range(2)]
    XS = [P.slot(f"xs{i}") for i in range(3)]
    XSD = [P.slot(f"xsd{i}") for i in range(8)]
    CSL = {k: P.slot("c_" + k) for k in ["cvec", "bp", "sk", "mask", "ws", "sel", "brow", "ident", "m01"]}
    C = {k: Buf("c_" + k) for k in ["cvec", "mask", "ws", "bp", "es", "sel", "ones", "brow", "hb", "sk", "ek", "eps", "ident", "m01"]}
    QB = [[Buf(f"q{c}_{j}") for j in range(8)] for c in range(4)]
    UB = [[Buf(f"u{c}_{j}") for j in range(8)] for c in range(4)]
    GZB = [Buf(f"gz{j}") for j in range(8)]
    VNB = [Buf(f"vn{j}") for j in range(8)]
    KB = [[Buf(f"k{v}_{b}") for b in range(17)] for v in range(2)]
    VB = [Buf(f"v{b}") for b in range(17)]
    YRB = [[Buf(f"yr{k}_{b}") for b in range(4)] for k in range(8)]
    YNB = [Buf(f"yn{k}") for k in range(8)]
    PTB = [[Buf(f"pt{s}_{k}") for k in range(2)] for s in range(2)]
    RDB = [Buf(f"rd{i}") for i in range(2)]
    LNB = Buf("lnst")
    MVB = Buf("mv")
    LRB = Buf("lrs")
    FENCE = Buf("fence")
    DUM = Buf("dummy")

    rr = {}

    def nxt(key, n):
        v = rr.get(key, 0)
        rr[key] = (v + 1) % n
        return v

    def bank():
        i = nxt("bank", 8)
        return ps[i], PB[i]

    def blks(t0, n):
        return range(t0 // 128, (t0 + n) // 128)

    def mm(out, lhsT, rhs, start, stop):
        return lambda h: h.matmul(out, lhsT, rhs, start=start, stop=stop)

    def act_(out, in_, func, **kw):
        return lambda h: h.activation(out, in_, func, **kw)

    def stt(out, in0, scalar, in1, op0, op1):
        return lambda h: h.scalar_tensor_tensor(out, in0, scalar, in1, op0, op1)

    def tt(out, in0, in1, op):
        return lambda h: h.tensor_tensor(out, in0, in1, op)

    def ts(out, in0, s1, s2, op0, op1):
        return lambda h: h.tensor_scalar(out, in0, s1, s2, op0, op1)

    def dma(out, in_):
        return lambda h: h.dma_start(out=out, in_=in_)

    def fence_acquire():
        P.op("dve", lambda h: h.memset(dummy_sb[:, 0:1], 0.0), writes=[FENCE, DUM])

    def load_unit(pieces):
        def last_use(i):
            m = -1
            for t in WB[i].readers + WB[i].writers:
                if t[0] == "c" and t[1] == "pe":
                    m = max(m, t[2])
                elif t[0] == "d":
                    m = max(m, rr.get("slot_stamp%d" % i, -1))
            return m
        s = min(range(NSLOT), key=last_use)
        rr["slot_stamp%d" % s] = len(P.ops["pe"])
        first = True
        for (off, c, w, src) in pieces:
            dst = wst[:, s, off:off + c * w].rearrange("p (c w) -> p c w", c=c)
            P.op("pool", dma(dst, src), writes=[WB[s]] if first else [], join=[] if first else [WB[s]], dma=WS[s])
            first = False
        return s

    P.op("sp", dma(cvec_sb[:, :], cvec_d), writes=[C["cvec"]], dma=CSL["cvec"])
    P.op("sp", dma(bp_sb[:, :, :], bsrep_d), writes=[C["bp"]], dma=CSL["bp"])
    P.op("sp", dma(sk_sb[:, :], sinks_d), writes=[C["sk"]], dma=CSL["sk"])
    P.op("pool", dma(mask_sb[:, :, :], masks_d), writes=[C["mask"]], dma=CSL["mask"])
    P.op("pool", dma(ws_sb[:, :, :], wsT_d), writes=[C["ws"]], dma=CSL["ws"])
    P.op("pool", dma(sel_sb[:, :], sel_d), writes=[C["sel"]], dma=CSL["sel"])
    P.op("pool", dma(ident_sb[:, :], ident_d), writes=[C["ident"]], dma=CSL["ident"])
    xTv = xT_d.rearrange("(c p) t -> p c t", p=128)

    def load_x(st_, tiles):
        for i, (t0, n) in enumerate(tiles):
            c0 = t0 + st_ * 1024
            P.op("sp", dma(x_sb[:, :, t0:t0 + n], xTv[:, :, c0:c0 + n]),
                 writes=[XB[d][b] for d in range(8) for b in blks(t0, n)], dma=XS[i])

    P.op("dve", lambda h: h.memset(ones_sb[:, :], 1.0), writes=[C["ones"]])
    P.op("dve", lambda h: h.memset(eps_sb[:, :], EPS), writes=[C["eps"]])
    P.op("dve", lambda h: h.memset(brow_sb[:, :], 0.0), writes=[C["brow"]])
    P.op("pool", dma(brow_sb[0:1, :], binB_d), writes=[C["brow"]], dma=CSL["brow"])

    def late_setup():
        P.op("dve", lambda h: h.memset(es_sb[:, :, :], 0.0), writes=[C["es"]])
        P.op("dve", ts(hb_sb[:, 0:10], cvec_sb[:, BA:BA + 10], 0.5, None, ALU.mult, ALU.bypass),
             reads=[C["cvec"]], writes=[C["hb"]])
        P.op("dve", ts(hb_sb[:, 10:20], cvec_sb[:, BA:BA + 10], GS, None, ALU.mult, ALU.bypass),
             reads=[C["cvec"], C["hb"]], writes=[C["hb"]])
        P.op("pool", dma(sq_sb[:, 0, :], m01_d), writes=[SQB[0]], dma=CSL["m01"])
        for hlf in range(2):
            wv_ = ws_sb[:, 4 * hlf:4 * hlf + 4, :]
            P.op("dve", tt(wv_, wv_, sq_sb[:, 0, :].rearrange("p (g t) -> p g t", g=4), ALU.mult),
                 reads=[C["ws"], SQB[0]], writes=[C["ws"]])
        P.op("act", act_(ek_sb[:, :], sk_sb[:, :], AF.Exp), reads=[C["sk"]], writes=[C["ek"]])
        for g in range(2):
            P.op("dve", ts(es_sb[0:4, g, :], ones_sb[0:4, 0:64], ek_sb[0:4, g:g + 1], None, ALU.mult, ALU.bypass),
                 reads=[C["ek"], C["ones"], C["es"]], writes=[C["es"]])
        for g in range(8):
            c, gi = divmod(g, 2)
            pr, PR = bank()
            P.op("pe", mm(pr[:, 0:128], ones_sb[:, :], ws_sb[:, g, :], True, True),
                 reads=[C["ones"], C["ws"]], writes=[PR])
            sl = slice(gi * 64, (gi + 1) * 64)
            P.op("dve", stt(bp_sb[sl, c, :], pr[sl, 0:128], cvec_sb[sl, LB + c:LB + c + 1], bp_sb[sl, c, :],
                            ALU.mult, ALU.add),
                 reads=[PR, C["cvec"], C["bp"]], writes=[C["bp"]])

    def norm_stats(t0, n, src_rows, scale, use_dve=False):
        pq, PQ = bank()
        k = len(src_rows)
        for i, (ap, bufs) in enumerate(src_rows):
            qi = nxt("sq", 3)
            if use_dve and i % 2 == 1:
                P.op("dve", tt(sq_sb[:, qi, 0:n], ap, ap, ALU.mult), reads=bufs, writes=[SQB[qi]])
            else:
                P.op("act", act_(sq_sb[:, qi, 0:n], ap, AF.Square), reads=bufs, writes=[SQB[qi]])
            P.op("pe", mm(pq[:, 0:n], ones_sb[:, :], sq_sb[:, qi, 0:n], i == 0, i == k - 1),
                 reads=[SQB[qi], C["ones"]], writes=[PQ])
        ri = nxt("rs", 2)
        P.op("act", act_(rs_sb[:, ri, 0:n], pq[:, 0:n], AF.Ln, scale=scale, bias=eps_sb[:, 0:1]), reads=[PQ, C["eps"]],
             writes=[RSB[ri]])
        P.op("act", act_(rs_sb[:, ri, 0:n], rs_sb[:, ri, 0:n], AF.Exp, scale=-0.5), reads=[RSB[ri]],
             writes=[RSB[ri]])
        return ri

    def xrows(t0, n):
        return [(x_sb[:, d, t0:t0 + n], [XB[d][b] for b in blks(t0, n)]) for d in range(8)]

    def norm_to_h(gcol, tiles, use_dve=True):
        for (t0, n) in tiles:
            ri = norm_stats(t0, n, xrows(t0, n), 1.0 / 1024, use_dve)
            for d in range(8):
                P.op("dve", stt(h_sb[:, d, t0:t0 + n], x_sb[:, d, t0:t0 + n], cvec_sb[:, gcol + d:gcol + d + 1],
                                rs_sb[:, ri, 0:n], ALU.mult, ALU.mult),
                     reads=[XB[d][b] for b in blks(t0, n)] + [RSB[ri], C["cvec"]],
                     writes=[HB[d][b] for b in blks(t0, n)])

    h32 = h_sb.bitcast(F32)
    HALL = [HB[d][b] for d in range(8) for b in range(9)]
    OSH = [P.slot(f"osh{i}") for i in range(8)]

    def final_norm(st_, tiles, last=False):
        for (t0, n) in tiles:
            ri = norm_stats(t0, n, xrows(t0, n), 1.0 / 1024, True)
            for d in range(8):
                col0 = st_ * 1024 + (t0 - 128)
                if last:
                    stg = h32[:, d, 0:n]
                    P.op("dve", stt(stg, x_sb[:, d, t0:t0 + n], cvec_sb[:, GF + d:GF + d + 1],
                                    rs_sb[:, ri, 0:n], ALU.mult, ALU.mult),
                         reads=[XB[d][b] for b in blks(t0, n)] + [RSB[ri], C["cvec"]],
                         writes=[HB[d][b] for b in range(9)])
                    P.op("sp", dma(outT_d[d * 128:(d + 1) * 128, col0:col0 + n], stg),
                         reads=[HB[d][b] for b in range(9)], dma=OSH[d])
                    continue
                oi = nxt("os", 2)
                P.op("dve", stt(ost_sb[:, oi, 0:n], x_sb[:, d, t0:t0 + n], cvec_sb[:, GF + d:GF + d + 1],
                                rs_sb[:, ri, 0:n], ALU.mult, ALU.mult),
                     reads=[XB[d][b] for b in blks(t0, n)] + [RSB[ri], C["cvec"]], writes=[OSB[oi]])
                P.op("sp", dma(outT_d[d * 128:(d + 1) * 128, col0:col0 + n], ost_sb[:, oi, 0:n]),
                     reads=[OSB[oi]], dma=OS[oi])

    def ffn_A(fi, tiles, hooks=None, lead=NSLOT):
        hooks = hooks or {}
        wgv = wg_d[fi].rearrange("(c p) f -> p c f", p=128)
        wuv = wu_d[fi].rearrange("(c p) f -> p c f", p=128)

        def unit_load(u):
            return load_unit([(0, 8, 256, wgv[:, :, u * 256:(u + 1) * 256]),
                              (2048, 8, 256, wuv[:, :, u * 256:(u + 1) * 256])])

        def unit_compute(u, s, tl):
            wgs = wst[:, s, 0:2048].rearrange("p (c w) -> p c w", c=8)
            wus = wst[:, s, 2048:4096].rearrange("p (c w) -> p c w", c=8)
            for k in range(2):
                lf = 2 * u + k
                for (t0, n) in tl:
                    bl = list(blks(t0, n))
                    pg, PG = bank()
                    pu, PU = bank()
                    for d in range(8):
                        P.op("pe", mm(pg[:, 0:n], wgs[:, d, k * 128:(k + 1) * 128], h_sb[:, d, t0:t0 + n], d == 0, d == 7),
                             reads=[WB[s]] + [HB[d][b] for b in bl], writes=[PG])
                    for d in range(8):
                        P.op("pe", mm(pu[:, 0:n], wus[:, d, k * 128:(k + 1) * 128], h_sb[:, d, t0:t0 + n], d == 0, d == 7),
                             reads=[WB[s]] + [HB[d][b] for b in bl], writes=[PU])
                    si = nxt("si", 3)
                    P.op("act", act_(silu_sb[:, si, 0:n], pg[:, 0:n], AF.Silu), reads=[PG],
                         writes=[SIB[si], STB[si // 2]])
                    P.op("dve", tt(actv(lf, t0, n), silu_sb[:, si, 0:n], pu[:, 0:n], ALU.mult),
                         reads=[SIB[si], STB[si // 2], PU, FENCE], writes=[AB[lf][b] for b in bl])

        head = list(range(lead))
        slots = {u: unit_load(u) for u in head}
        for ti, tile in enumerate(tiles):
            for u in head:
                unit_compute(u, slots[u], [tile])
                if (ti, u) in hooks:
                    hooks[(ti, u)]()
        for u in range(lead, 11):
            unit_compute(u, unit_load(u), tiles)

    def ffn_B(fi, tiles, post, post_late=None, split=3):
        wdv = wd_d[fi].rearrange("(c p) m -> p c m", p=128)

        def load(dg):
            return load_unit([(0, 22, 256, wdv[:, :, dg * 256:(dg + 1) * 256])])

        def compute(dg, s, tl, mid_hook=None):
            wds = wst[:, s, 0:5632].rearrange("p (c w) -> p c w", c=22)
            for k in range(2):
                if k == 1 and mid_hook is not None:
                    mid_hook()
                dm = 2 * dg + k
                for (t0, n) in tl:
                    bl = list(blks(t0, n))
                    po, PO = bank()
                    for lf in range(22):
                        P.op("pe", mm(po[:, 0:n], wds[:, lf, k * 128:(k + 1) * 128], actv(lf, t0, n), lf == 0, lf == 21),
                             reads=[WB[s], FENCE] + [AB[lf][b] for b in bl], writes=[PO])
                    xb_ = [XB[dm][b] for b in bl]
                    P.op("dve", stt(x_sb[:, dm, t0:t0 + n], po[:, 0:n], 0.5, x_sb[:, dm, t0:t0 + n], ALU.mult, ALU.add),
                         reads=[PO] + xb_, writes=xb_)

        for dg in range(4 - split):
            compute(dg, load(dg), tiles)
        last = list(range(4 - split, 4))
        slots = {dg: load(dg) for dg in last}
        pending = None
        for ti, tile in enumerate(tiles):
            for i, dg in enumerate(last):
                if pending is not None and i == 0:
                    compute(dg, slots[dg], [tile], mid_hook=lambda p_=pending: post(p_))
                else:
                    compute(dg, slots[dg], [tile])
                if pending is not None and i == len(last) - 1 and post_late is not None:
                    post_late(pending)
            pending = ti
        return pending

    GS = (2 * 0.044715) ** 0.5

    def gelu(dst, src, n, sbias, hbias, src_bufs, extra_reads, dst_bufs):
        gi = nxt("gt", 2)
        zt = zt_sb[:, gi, 0:n]
        sv = st_sb[:, gi, 0:n]
        if sbias is None:
            P.op("act", act_(zt, src, AF.Identity, scale=0.5), reads=src_bufs, writes=[ZTB[gi]])
            P.op("act", act_(sv, src, AF.Square, scale=GS), reads=src_bufs, writes=[STB[gi]])
        else:
            P.op("act", act_(zt, src, AF.Identity, scale=0.5, bias=hbias), reads=src_bufs + [C["hb"]], writes=[ZTB[gi]])
            P.op("act", act_(sv, src, AF.Square, scale=GS, bias=sbias), reads=src_bufs + [C["hb"]], writes=[STB[gi]])
        P.op("dve", stt(sv, sv, 2.0, zt, ALU.add, ALU.mult), reads=[STB[gi], ZTB[gi]], writes=[STB[gi]])
        P.op("act", act_(sv, sv, AF.Tanh, scale=0.7978845608028654), reads=[STB[gi]], writes=[STB[gi]])
        P.op("dve", stt(dst, sv, 1.0, zt, ALU.add, ALU.mult), reads=[STB[gi], ZTB[gi]] + extra_reads, writes=dst_bufs)

    def mixer(st_, tiles_all, own, pre=None):
        fence_acquire()
        winAv = winA_d.rearrange("(c p) f -> p c f", p=128)
        winBv = winB_d.rearrange("(c p) f -> p c f", p=128)
        state = {"pre": pre}

        def loadA(u):
            return load_unit([(0, 8, 256, winAv[:, :, u * 256:(u + 1) * 256])])

        def projA(ci, s, k, tl):
            wa = wst[:, s, 0:2048].rearrange("p (c w) -> p c w", c=8)
            for (t0, n) in tl:
                bl = list(blks(t0, n))
                pb, PBk = bank()
                for d in range(8):
                    P.op("pe", mm(pb[:, 0:n], wa[:, d, k * 128:(k + 1) * 128], h_sb[:, d, t0:t0 + n], d == 0, d == 7),
                         reads=[WB[s]] + [HB[d][b] for b in bl], writes=[PBk])
                if state["pre"] is not None:
                    state["pre"]()
                    state["pre"] = None
                bcol = cvec_sb[:, BA + ci:BA + ci + 1]
                if ci < 4:
                    k0 = t0 - 128
                    P.op("act", act_(q3[:, ci, k0:k0 + n], pb[:, 0:n], AF.Identity, bias=bcol),
                         reads=[PBk, C["cvec"], FENCE], writes=[QB[ci][j] for j in blks(k0, n)])
                elif ci < 6:
                    var = ci - 4
                    gc = t0 + st_ * 1024
                    P.op("act", act_(kT_sb[:, var, gc:gc + n], pb[:, 0:n], AF.Identity, bias=bcol),
                         reads=[PBk, C["cvec"]], writes=[KB[var][b] for b in blks(gc, n)])
                else:
                    c = ci - 6
                    k0 = t0 - 128
                    gelu(u3[:, c, k0:k0 + n], pb[:, 0:n], n, hb_sb[:, 10 + ci:11 + ci], hb_sb[:, ci:ci + 1], [PBk], [FENCE],
                         [UB[c][j] for j in blks(k0, n)])

        for u in range(3):
            s = loadA(u)
            for k in range(2):
                ci = 2 * u + k
                projA(ci, s, k, tiles_all if ci in (4, 5) else own)
        s_v = load_unit([(0, 8, 128, winBv[:, :, 512:640])])
        wv = wst[:, s_v, 0:1024].rearrange("p (c w) -> p c w", c=8)
        xblocks = sorted({b for (t0, n) in tiles_all for b in blks(t0, n)})
        for xb in xblocks:
            gb = xb + 8 * st_
            tsl = slice(xb * 128, (xb + 1) * 128)
            pv_, PV = bank()
            for d in range(8):
                P.op("pe", mm(pv_[:, 0:128], h_sb[:, d, tsl], wv[:, d, 0:128], d == 0, False),
                     reads=[WB[s_v], HB[d][xb]], writes=[PV])
            P.op("pe", mm(pv_[:, 0:128], ones_sb[:, 0:128], brow_sb[:, 512:640], False, True),
                 reads=[C["ones"], C["brow"]], writes=[PV])
            P.op("act", act_(v_sb[:, gb, :], pv_[:, 0:128], AF.Copy), reads=[PV], writes=[VB[gb]])
        s_u = [loadA(3), loadA(4)]
        s_zv = load_unit([(0, 8, 512, winBv[:, :, 0:512])])
        wz = wst[:, s_zv, 0:4096].rearrange("p (c w) -> p c w", c=8)

        def zv_block(j):
            xb = j + 1
            tsl = slice(xb * 128, (xb + 1) * 128)
            pz, PZ = bank()
            for d in range(8):
                P.op("pe", mm(pz[:, 0:512], h_sb[:, d, tsl], wz[:, d, 0:512], d == 0, False),
                     reads=[WB[s_zv], HB[d][xb]], writes=[PZ])
            P.op("pe", mm(pz[:, 0:512], ones_sb[:, 0:128], brow_sb[:, 0:512], False, True),
                 reads=[C["ones"], C["brow"]], writes=[PZ])
            gelu(gz3[:, j, :], pz[:, 0:512], 512, None, None, [PZ], [FENCE], [GZB[j]])
            P.op("dve", lambda h, j=j: h.bn_stats(lnst_sb[:, j, :], gz3[:, j, :]),
                 reads=[GZB[j], FENCE], join=[LNB])
            P.op("dve", lambda h, j=j: h.bn_aggr(mv_sb[:, j, :], lnst_sb[:, j, :]), reads=[LNB], join=[MVB])

        def ln_finish(j0):
            P.op("act", act_(lrs_sb[:, j0:j0 + 4], mv_sb[:, j0:j0 + 4, 1], AF.Ln, bias=eps_sb[:, 0:1]),
                 reads=[MVB, C["eps"]], join=[LRB])
            P.op("act", act_(lrs_sb[:, j0:j0 + 4], lrs_sb[:, j0:j0 + 4], AF.Exp, scale=-0.5), reads=[LRB], join=[LRB])
            for j in range(j0, j0 + 4):
                P.op("dve", ts(vn3[:, j, :], gz3[:, j, :], mv_sb[:, j, 0:1], lrs_sb[:, j:j + 1], ALU.subtract, ALU.mult),
                     reads=[GZB[j], MVB, LRB, FENCE], writes=[VNB[j]])

        def scores(j):
            gb = j + 1 + 8 * st_
            k0 = j * 128
            pset = j % 2
            for kb in range(2):
                kgb = gb - 1 + kb
                mi = 0 if kb == 1 else (2 if kgb == 0 else 1)
                for par in range(2):
                    psl = slice(par * 64, (par + 1) * 64)
                    pc, PC = bank()
                    P.op("pe", mm(pc[:, 0:512], ident_sb[:, :], mask_sb[:, mi, :], True, False),
                         reads=[C["ident"], C["mask"]], writes=[PC])
                    for g in range(2):
                        var = 0 if g == par else 1
                        P.op("pe", mm(pc[:, g * 256:(g + 1) * 256], kT_sb[psl, var, kgb * 128:(kgb + 1) * 128],
                                      q3[psl, 2 * g:2 * g + 2, k0:k0 + 128], False, g == 1),
                             reads=[KB[var][kgb], QB[2 * g][j], QB[2 * g + 1][j], FENCE], writes=[PC])
                    P.op("act", act_(pt6[:, pset, kb, :, par, :], pc[:, 0:512].rearrange("p (g w) -> p g w", g=2),
                                     AF.Exp, scale=0.125),
                         reads=[PC, FENCE], writes=[PTB[pset][kb]])

        def gmlp(j):
            bi = j % 4
            k0 = j * 128
            bsl = slice(bi * 128, (bi + 1) * 128)
            pm, PM = bank()
            pm3 = pm[:, 0:512].rearrange("p (c t) -> p c t", c=4)
            for c in range(4):
                for gi in range(2):
                    g = 2 * c + gi
                    P.op("pe", mm(pm3[gi * 64:(gi + 1) * 64, c, :], vn3[:, j, g * 64:(g + 1) * 64], ws_sb[:, g, :], True, True),
                         reads=[VNB[j], C["ws"], FENCE], writes=[PM])
            for c in range(4):
                P.op("dve", stt(yr3[:, 4 + c, bsl], pm3[:, c, :], cvec_sb[:, LG + c:LG + c + 1], bp_sb[:, c, :],
                                ALU.mult, ALU.add),
                     reads=[PM, C["cvec"], C["bp"], FENCE], writes=[YRB[4 + c][bi]])
            P.op("dve", tt(yr3[:, 4:8, bsl], yr3[:, 4:8, bsl], u3[:, :, k0:k0 + 128], ALU.mult),
                 reads=[YRB[4 + c][bi] for c in range(4)] + [UB[c][j] for c in range(4)],
                 writes=[YRB[4 + c][bi] for c in range(4)])

        def pv(j):
            gb = j + 1 + 8 * st_
            pset = j % 2
            bi = j % 4
            bsl = slice(bi * 128, (bi + 1) * 128)
            po, PO = bank()
            pd, PD = bank()
            for g in range(2):
                gsl = slice(g * 64, (g + 1) * 64)
                for kb in range(2):
                    kgb = gb - 1 + kb
                    P.op("pe", mm(po[gsl, 0:512], v_sb[:, kgb, gsl], pt5[:, pset, kb, g, :], kb == 0, kb == 1),
                         reads=[VB[kgb], PTB[pset][kb], FENCE], writes=[PO])
                for kb in range(2):
                    P.op("pe", mm(pd[gsl, 0:512], ones_sb[:, 0:64], pt5[:, pset, kb, g, :], kb == 0, False),
                         reads=[C["ones"], PTB[pset][kb], FENCE], writes=[PD])
                P.op("pe", mm(pd[gsl, 0:512], es_sb[:, g, :], sel_sb[:, :], False, True),
                     reads=[C["es"], C["sel"]], writes=[PD])
            ri2 = j % 2
            P.op("dve", lambda h: h.reciprocal(rd3[:, ri2, :], pd[:, 0:512]), reads=[PD, FENCE], writes=[RDB[ri2]])
            P.op("dve", tt(yr3[:, 0:4, bsl], po[:, 0:512].rearrange("p (r t) -> p r t", r=4),
                           rd3[:, ri2, :].rearrange("p (r t) -> p r t", r=4), ALU.mult),
                 reads=[PO, RDB[ri2], FENCE], writes=[YRB[r][bi] for r in range(4)])

        def ynorm():
            ris = []
            for half in range(2):
                rows = [(yr3[:, half * 4 + r, :], [YRB[half * 4 + r][b] for b in range(4)] + [FENCE]) for r in range(4)]
                ris.append(norm_stats(0, 512, rows, 1.0 / 512))
            for half in range(2):
                for r in range(4):
                    kk = half * 4 + r
                    gcol = (GA if half == 0 else GG) + r
                    P.op("dve", stt(yn3[:, kk, :], yr3[:, kk, :], cvec_sb[:, gcol:gcol + 1], rs_sb[:, ris[half], :],
                                    ALU.mult, ALU.mult),
                         reads=[YRB[kk][b] for b in range(4)] + [RSB[ris[half]], C["cvec"], FENCE], writes=[YNB[kk]])

        wo_slots = {}

        def wout_group(ti, dm):
            dh, k = divmod(dm, 4)
            if dh not in wo_slots:
                wo_slots[dh] = load_unit([(0, 8, 512, woutL_d[:, :, dh * 512:(dh + 1) * 512])])
            s = wo_slots[dh]
            wo = wst[:, s, 0:4096].rearrange("p (c w) -> p c w", c=8)
            t0x = 128 + ti * 512
            bl = list(blks(t0x, 512))
            po2, PO2 = bank()
            for kk in range(8):
                P.op("pe", mm(po2[:, 0:512], wo[:, kk, k * 128:(k + 1) * 128], yn3[:, kk, :], kk == 0, kk == 7),
                     reads=[WB[s], YNB[kk], FENCE], writes=[PO2])
            xb_ = [XB[dm][b] for b in bl]
            P.op("dve", stt(x_sb[:, dm, t0x:t0x + 512], po2[:, 0:512], cvec_sb[:, BO + dm:BO + dm + 1],
                            x_sb[:, dm, t0x:t0x + 512], ALU.add, ALU.add),
                 reads=[PO2, C["cvec"]] + xb_, writes=xb_)

        scores(0)
        for j in range(8):
            if j + 1 < 8:
                scores(j + 1)
            if j < 4:
                c = j
                projA(6 + c, s_u[c // 2], c % 2, own)
                zv_block(2 * j)
                zv_block(2 * j + 1)
            else:
                gmlp(j)
                wout_group(0, 2 * (j - 4))
                wout_group(0, 2 * (j - 4) + 1)
            pv(j)
            if j == 3:
                ln_finish(0)
                ln_finish(4)
                for jj in range(4):
                    gmlp(jj)
                ynorm()
        norm_to_h(G2, [own[0]], use_dve=False)
        ynorm()
        for dm in range(8):
            wout_group(1, dm)
        norm_to_h(G2, [own[1]], use_dve=False)

    own = [(128, 512), (640, 512)]
    t_all0 = [(0, 384), (384, 384), (768, 384)]
    finals = list(OS) + list(OSH)
    for d in range(8):
        P.op("sp", dma(x_sb[:, d, 0:384], xT_d[d * 128:(d + 1) * 128, 0:384]),
             writes=[XB[d][b] for b in blks(0, 384)], dma=XSD[d])
    for i in (1, 2):
        t0, n = t_all0[i]
        P.op("sp", dma(x_sb[:, :, t0:t0 + n], xTv[:, :, t0:t0 + n]),
             writes=[XB[d][b] for d in range(8) for b in blks(t0, n)], dma=XS[i])
    norm_to_h(G1, t_all0)
    hooks = {}
    for st_ in range(2):
        tiles_all = t_all0 if st_ == 0 else own
        ffn_A(0, tiles_all, hooks)
        if st_ == 0:
            late_setup()
        pend = ffn_B(0, tiles_all, post=lambda ti: norm_to_h(GM, [tiles_all[ti]]))
        mixer(st_, tiles_all, own, pre=lambda: norm_to_h(GM, [tiles_all[pend]]))
        fence_acquire()
        ffn_A(1, own)

        def postA(ti, st_=st_):
            final_norm(st_, [own[ti]], last=(st_ == 1 and ti == 1))
            if st_ == 0:
                t0, n = own[ti]
                for d in range(8):
                    P.op("sp", dma(x_sb[:, d, t0:t0 + n], xT_d[d * 128:(d + 1) * 128, 1024 + t0:1024 + t0 + n]),
                         writes=[XB[d][b] for b in blks(t0, n)], dma=XSD[d])

        def postB(ti, st_=st_):
            if st_ == 0:
                norm_to_h(G1, [own[ti]])

        pend2 = ffn_B(1, own, post=postA, post_late=postB)
        if st_ == 0:
            hooks = {(0, 0): (lambda: postA(pend2)), (0, 2): (lambda: postB(pend2))}
        else:
            postA(pend2)
    P.emit(nc, final_waits=finals)
    return nc


def make_in_maps(inp):
    f = lambda k: np.asarray(inp[k], dtype=np.float32)
    x = f("x")
    w_in = f("w_in")[0]
    b_in = f("b_in")[0]
    w_out = f("w_out")[0]
    qcols = []
    for g in range(2):
        for i in range(2):
            h0, h1 = 4 * g + i, 4 * g + 2 + i
            qcols += list(range(h0 * 64, h0 * 64 + 64)) + list(range(h1 * 64, h1 * 64 + 64))
    colsA = qcols + list(range(512, 640)) + list(range(576, 640)) + list(range(512, 576)) + list(range(768, 1280))
    colsB = list(range(1280, 1792)) + list(range(640, 768))
    winA = np.ascontiguousarray(w_in[:, colsA])
    winB = np.ascontiguousarray(w_in[:, colsB])
    binA = b_in[colsA]
    binB = np.ascontiguousarray(b_in[colsB][None, :])
    pc = lambda v, n: np.ascontiguousarray(v.reshape(n, 128).T)
    attn_g = f("attn_out_norm_g")[0]
    cvec = np.concatenate([
        pc(f("ffn1_norm_g")[0], 8), pc(f("mix_norm_g")[0], 8), pc(f("ffn2_norm_g")[0], 8), pc(f("final_norm_g"), 8),
        pc(binA, 10), pc(f("b_out")[0], 8),
        np.ascontiguousarray(attn_g.reshape(2, 4, 64).transpose(0, 2, 1).reshape(128, 4)),
        pc(f("gmlp_out_norm_g")[0], 4), pc(f("gmlp_ln_g")[0], 4), pc(f("gmlp_ln_b")[0], 4)], axis=1)
    cvec = np.ascontiguousarray(cvec.astype(np.float32))
    assert cvec.shape == (128, NCV)
    woA = w_out[:512].reshape(2, 4, 64, 1024).transpose(0, 2, 1, 3).reshape(128, 4, 1024)
    woG = w_out[512:].reshape(4, 128, 1024).transpose(1, 0, 2)
    woutL = np.ascontiguousarray(np.concatenate([woA, woG], axis=1))
    wsT = np.ascontiguousarray(f("gmlp_w_s")[0].transpose(2, 0, 1))
    b_s = f("gmlp_b_s")[0]
    bsrep = np.ascontiguousarray(np.repeat(b_s, 64, axis=0).reshape(4, 128, 128).transpose(1, 0, 2))
    kk = np.arange(128)[:, None]
    qq = np.arange(128)[None, :]
    NEG = np.float32(-30000.0)
    m01 = np.tile((qq >= kk).astype(np.float32), (1, 4))
    mC = np.tile(np.where(qq >= kk, np.float32(0.0), NEG).astype(np.float32), (1, 4))
    mP = np.tile(np.where(kk > qq, np.float32(0.0), NEG).astype(np.float32), (1, 4))
    sinks4 = np.ascontiguousarray(f("attn_sinks")[0].reshape(2, 4).T)
    sel = np.zeros((128, 512), np.float32)
    for r in range(4):
        sel[r, r * 128:(r + 1) * 128] = 1.0
    common = {
        "wg1": f("ffn1_w_gate")[0], "wu1": f("ffn1_w_up")[0], "wd1": f("ffn1_w_down")[0],
        "wg2": f("ffn2_w_gate")[0], "wu2": f("ffn2_w_up")[0], "wd2": f("ffn2_w_down")[0],
        "winA": winA, "winB": winB, "binB": binB, "woutL": woutL, "cvec": cvec, "wsT": wsT, "bsrep": bsrep,
        "sinks4": sinks4, "sel": sel, "ident": np.eye(128, dtype=np.float32), "m01": m01,
    }
    common = {k: np.ascontiguousarray(v, dtype=np.float32) for k, v in common.items()}
    in_maps = []
    for c in range(8):
        b, qd = divmod(c, 4)
        s0 = qd * 2048
        xs = np.zeros((NT, 1024), np.float32)
        xs[128:] = x[b, s0:s0 + 2048]
        if qd > 0:
            xs[:128] = x[b, s0 - 128:s0]
        m = dict(common)
        m["xT"] = np.ascontiguousarray(xs.T)
        m["masks"] = np.ascontiguousarray(np.stack([mC, mP, mP if qd > 0 else np.full_like(mP, NEG)], axis=1))
        in_maps.append(m)
    return in_maps


def kernel(**inputs):
    in_maps = make_in_maps(inputs)
    nc = build_program()
    res = run_bass_kernel_spmd(nc, in_maps, core_ids=list(range(8)))
    out = np.empty((2, 8192, 1024), np.float32)
    for c in range(8):
        b, qd = divmod(c, 4)
        out[b, qd * 2048:(qd + 1) * 2048, :] = np.asarray(res.results[c]["outT"], dtype=np.float32).T
    return out
```
